# Optimizing a Trainium2 kernel written in Bass

```python
import jax, jax.numpy as jnp
from jax import lax
import numpy as np

D_MODEL = 1024
BATCH = 1
SEQ = 16384
DEPTH = 4

N_MIXERS = 2
N_MLA = (DEPTH + 1) // 2
N_GLA = DEPTH // 2

MLA_HEADS = 8
QK_NOPE = 128
QK_ROPE = 64
V_HEAD = 128
Q_LORA = 384
KV_LORA = 256
ROPE_THETA = 10000.0
Q_BLOCK = 128

GLA_HEADS = 4
GLA_DK = D_MODEL // 2
GLA_DV = D_MODEL
GLA_HEAD_K = GLA_DK // GLA_HEADS
GLA_HEAD_V = GLA_DV // GLA_HEADS
GATE_RANK = 16
GATE_NORMALIZER = 16.0
CHUNK = 64

D_FF = 4 * D_MODEL
EPS = 1e-6

kernel_name = "hybrid_mla_gla_sqrelu_trunk"


def rms_norm(x, gain):
    xf = x.astype(jnp.float32)
    y = xf * lax.rsqrt(jnp.mean(xf * xf, axis=-1, keepdims=True) + EPS)
    return (y * gain.astype(jnp.float32)).astype(x.dtype)


def rope_tables(positions):
    inv_freq = 1.0 / (ROPE_THETA ** (jnp.arange(0, QK_ROPE, 2, dtype=jnp.float32) / QK_ROPE))
    ang = positions.astype(jnp.float32)[..., None] * inv_freq
    return jnp.cos(ang), jnp.sin(ang)


def apply_rope(t, cos, sin):
    tf = t.astype(jnp.float32)
    half = QK_ROPE // 2
    t1, t2 = tf[..., :half], tf[..., half:]
    out = jnp.concatenate([t1 * cos - t2 * sin, t2 * cos + t1 * sin], axis=-1)
    return out.astype(t.dtype)


def mla_mixer(xn, positions, w_in, q_norm, w_uq, kv_norm, w_ukv, w_o):
    B, S, _ = xn.shape
    h = xn @ w_in
    c_q = h[..., :Q_LORA]
    c_kv = h[..., Q_LORA:Q_LORA + KV_LORA]
    k_rope = h[..., Q_LORA + KV_LORA:]
    q = (rms_norm(c_q, q_norm) @ w_uq).reshape(B, S, MLA_HEADS, QK_NOPE + QK_ROPE)
    kv = (rms_norm(c_kv, kv_norm) @ w_ukv).reshape(B, S, MLA_HEADS, QK_NOPE + V_HEAD)
    q_nope, q_rope = q[..., :QK_NOPE], q[..., QK_NOPE:]
    k_nope, v = kv[..., :QK_NOPE], kv[..., QK_NOPE:]
    cos, sin = rope_tables(positions)
    q_rope = apply_rope(q_rope, cos[:, :, None, :], sin[:, :, None, :])
    k_rope = apply_rope(k_rope, cos, sin)
    scale = (QK_NOPE + QK_ROPE) ** -0.5
    n_blocks = S // Q_BLOCK
    qn_b = q_nope.reshape(B, n_blocks, Q_BLOCK, MLA_HEADS, QK_NOPE).transpose(1, 0, 2, 3, 4)
    qr_b = q_rope.reshape(B, n_blocks, Q_BLOCK, MLA_HEADS, QK_ROPE).transpose(1, 0, 2, 3, 4)
    key_idx = jnp.arange(S)

    def attend_block(args):
        qn, qr, blk = args
        s = (jnp.einsum('bqhd,bkhd->bhqk', qn, k_nope).astype(jnp.float32)
             + jnp.einsum('bqhd,bkd->bhqk', qr, k_rope).astype(jnp.float32)) * scale
        q_idx = blk * Q_BLOCK + jnp.arange(Q_BLOCK)
        mask = key_idx[None, :] <= q_idx[:, None]
        s = jnp.where(mask, s, -jnp.inf)
        p = jax.nn.softmax(s, axis=-1).astype(v.dtype)
        return jnp.einsum('bhqk,bkhd->bqhd', p, v)

    o = lax.map(attend_block, (qn_b, qr_b, jnp.arange(n_blocks)))
    o = o.transpose(1, 0, 2, 3, 4).reshape(B, S, MLA_HEADS * V_HEAD)
    return o @ w_o


def gla_mixer(xn, w_in, w_gk_up, b_gk, g_norm, w_o):
    B, S, _ = xn.shape
    h = xn @ w_in
    q = h[..., :GLA_DK]
    k = h[..., GLA_DK:2 * GLA_DK]
    v = h[..., 2 * GLA_DK:2 * GLA_DK + GLA_DV]
    g = h[..., 2 * GLA_DK + GLA_DV:2 * GLA_DK + 2 * GLA_DV]
    a = h[..., 2 * GLA_DK + 2 * GLA_DV:]
    gk = jax.nn.log_sigmoid((a @ w_gk_up + b_gk).astype(jnp.float32)) / GATE_NORMALIZER
    nc = S // CHUNK

    def to_chunks(t, d):
        return t.astype(jnp.float32).reshape(B, nc, CHUNK, GLA_HEADS, d).transpose(1, 0, 3, 2, 4)

    qc = to_chunks(q, GLA_HEAD_K) * (GLA_HEAD_K ** -0.5)
    kc = to_chunks(k, GLA_HEAD_K)
    vc = to_chunks(v, GLA_HEAD_V)
    bc = jnp.cumsum(to_chunks(gk, GLA_HEAD_K), axis=3)
    causal = jnp.tril(jnp.ones((CHUNK, CHUNK), dtype=bool))

    def step(state, inp):
        q_, k_, v_, b_ = inp
        o_inter = jnp.einsum('bhcd,bhde->bhce', q_ * jnp.exp(b_), state)
        diff = b_[:, :, :, None, :] - b_[:, :, None, :, :]
        decay = jnp.exp(jnp.where(causal[:, :, None], diff, -jnp.inf))
        attn = jnp.einsum('bhtd,bhsd,bhtsd->bhts', q_, k_, decay)
        o_intra = jnp.einsum('bhts,bhse->bhte', attn, v_)
        b_last = b_[:, :, -1:, :]
        new_state = (state * jnp.exp(b_last[:, :, 0, :, None])
                     + jnp.einsum('bhsd,bhse->bhde', k_ * jnp.exp(b_last - b_), v_))
        return new_state, o_inter + o_intra

    state0 = jnp.zeros((B, GLA_HEADS, GLA_HEAD_K, GLA_HEAD_V), jnp.float32)
    _, o = lax.scan(step, state0, (qc, kc, vc, bc))
    o = o.transpose(1, 0, 3, 2, 4).reshape(B, S, GLA_HEADS, GLA_HEAD_V)
    o = rms_norm(o, g_norm)
    o = o * jax.nn.silu(g.astype(jnp.float32)).reshape(B, S, GLA_HEADS, GLA_HEAD_V)
    return o.reshape(B, S, GLA_DV).astype(xn.dtype) @ w_o


def sq_relu_mlp(xn, w_up, w_down):
    hid = jnp.square(jax.nn.relu(xn @ w_up))
    return hid @ w_down


def setup_inputs(seed: int = 0) -> dict:
    key = jax.random.key(seed)
    ks = jax.random.split(key, 20)

    def w(k, n, fan_in, fan_out):
        return jax.random.normal(k, (n, fan_in, fan_out), jnp.float32) * (fan_in ** -0.5)

    def gain(k, shape):
        return 1.0 + 0.01 * jax.random.normal(k, shape, jnp.float32)

    x = jax.random.normal(ks[0], (BATCH, SEQ, D_MODEL), jnp.float32)
    positions = jnp.broadcast_to(jnp.arange(SEQ, dtype=jnp.int32), (BATCH, SEQ))
    gla_in_width = 2 * GLA_DK + 2 * GLA_DV + GATE_RANK
    return {
        "x": x,
        "positions": positions,
        "norm_mix": gain(ks[1], (DEPTH, D_MODEL)),
        "norm_mlp": gain(ks[2], (DEPTH, D_MODEL)),
        "mla_w_in": w(ks[3], N_MLA, D_MODEL, Q_LORA + KV_LORA + QK_ROPE),
        "mla_q_norm": gain(ks[4], (N_MLA, Q_LORA)),
        "mla_w_uq": w(ks[5], N_MLA, Q_LORA, MLA_HEADS * (QK_NOPE + QK_ROPE)),
        "mla_kv_norm": gain(ks[6], (N_MLA, KV_LORA)),
        "mla_w_ukv": w(ks[7], N_MLA, KV_LORA, MLA_HEADS * (QK_NOPE + V_HEAD)),
        "mla_w_o": w(ks[8], N_MLA, MLA_HEADS * V_HEAD, D_MODEL),
        "gla_w_in": w(ks[9], N_GLA, D_MODEL, gla_in_width),
        "gla_w_gk_up": w(ks[10], N_GLA, GATE_RANK, GLA_DK),
        "gla_b_gk": 0.1 * jax.random.normal(ks[11], (N_GLA, GLA_DK), jnp.float32),
        "gla_g_norm": gain(ks[12], (N_GLA, GLA_HEAD_V)),
        "gla_w_o": w(ks[13], N_GLA, GLA_DV, D_MODEL),
        "mlp_w_up": w(ks[14], DEPTH, D_MODEL, D_FF),
        "mlp_w_down": w(ks[15], DEPTH, D_FF, D_MODEL),
        "final_norm": gain(ks[16], (D_MODEL,)),
    }


def reference(x, positions, norm_mix, norm_mlp, mla_w_in, mla_q_norm, mla_w_uq, mla_kv_norm,
              mla_w_ukv, mla_w_o, gla_w_in, gla_w_gk_up, gla_b_gk, gla_g_norm, gla_w_o,
              mlp_w_up, mlp_w_down, final_norm):
    for i in range(DEPTH):
        xn = rms_norm(x, norm_mix[i])
        j = i // N_MIXERS
        if i % N_MIXERS == 0:
            mix = mla_mixer(xn, positions, mla_w_in[j], mla_q_norm[j], mla_w_uq[j],
                            mla_kv_norm[j], mla_w_ukv[j], mla_w_o[j])
        else:
            mix = gla_mixer(xn, gla_w_in[j], gla_w_gk_up[j], gla_b_gk[j], gla_g_norm[j], gla_w_o[j])
        x = x + mix.astype(x.dtype)
        x = x + sq_relu_mlp(rms_norm(x, norm_mlp[i]), mlp_w_up[i], mlp_w_down[i]).astype(x.dtype)
    return rms_norm(x, final_norm)
```

```python
import math
import numpy as np
from contextlib import ExitStack
import concourse.bass as bass
import concourse.mybir as mybir
from concourse.bass_utils import run_bass_kernel_spmd

F32 = mybir.dt.float32
BF16 = mybir.dt.bfloat16
I32 = mybir.dt.int32
AF = mybir.ActivationFunctionType
ALU = mybir.AluOpType

NCORES = 8
S = 16384
DM = 1024
NT = S // NCORES
TB = 512
NB = NT // TB
EPS = 1e-6
DFF = 4096
QL, KVL, ROPE = 384, 256, 64
NOPE, VH = 128, 128
ATT_SCALE = (NOPE + ROPE) ** -0.5
GDK, GDV, GHK, GHV, GRANK = 512, 1024, 128, 256, 16
CH = 64

SAME_ENG_SYNC = True


class Res:
    __slots__ = ("name", "last_w", "readers", "dsem", "dcnt", "dkey")

    def __init__(self, name):
        self.name = name
        self.last_w = None
        self.readers = {}
        self.dsem = None
        self.dcnt = 0
        self.dkey = None


class Prog:
    def __init__(self, nc):
        self.nc = nc
        self.stacks = [ExitStack()]
        self.q = {e: [] for e in ("pe", "act", "dve", "pool", "sp")}
        self.sem = {}
        for e in ("pe", "act", "dve", "pool"):
            self.sem[e] = self.stacks[0].enter_context(nc.semaphore("sem_" + e))
        self.cnt = {e: 0 for e in self.sem}
        self.waited = {e: {} for e in self.q}
        self.nres = 0
        self.ndsem = 0
        self.out_toks = []
        self.dsem_free = []
        self.dsem_live = []
        self.dsem_live.append([])
        self.nalloc = 0

    def sb(self, name, shape, dt):
        self.nalloc += 1
        return self.stacks[-1].enter_context(self.nc.sbuf_tensor(f"{name}_{self.nalloc}", shape, dt))

    def ps(self, name, shape, dt=F32):
        self.nalloc += 1
        return self.stacks[-1].enter_context(self.nc.psum_tensor(f"{name}_{self.nalloc}", shape, dt))

    def res(self, name=None):
        self.nres += 1
        return Res(f"{name or 'r'}_{self.nres}")

    def barrier(self):
        toks = [(self.sem[e], e, self.cnt[e]) for e in self.sem if self.cnt[e] > 0]
        for lvl in self.dsem_live:
            for r in lvl:
                toks.append((r.dsem, r.dkey, r.dcnt))
        for eng in self.q:
            for (sem, key, val) in toks:
                if key == eng and eng != "sp":
                    continue
                if self.waited[eng].get(key, 0) >= val:
                    continue
                self.waited[eng][key] = val
                self.q[eng].append(lambda e, sem=sem, val=val: e.wait_ge(sem, val))

    def phase_begin(self):
        self.stacks.append(ExitStack())
        self.dsem_live.append([])

    def phase_end(self):
        self.barrier()
        for r in self.dsem_live.pop():
            self.dsem_free.append((r.dsem, r.dkey, r.dcnt))
            r.dsem = None
        self.stacks.pop().close()

    def _deps(self, eng, reads, writes):
        deps = []
        for r in reads:
            if r.last_w is not None:
                deps.append(r.last_w)
        for r in writes:
            if r.last_w is not None:
                deps.append(r.last_w)
            deps.extend(r.readers.values())
        for (sem, key, val, deng) in deps:
            if deng == eng and (eng == "pe" or not SAME_ENG_SYNC):
                continue
            if self.waited[eng].get(key, 0) >= val:
                continue
            self.waited[eng][key] = val
            self.q[eng].append(lambda e, sem=sem, val=val: e.wait_ge(sem, val))

    def _record(self, tok, reads, writes):
        for r in writes:
            r.last_w = tok
            r.readers = {}
        for r in reads:
            if r not in writes:
                r.readers[tok[1]] = tok

    def op(self, eng, fn, reads=(), writes=()):
        self._deps(eng, reads, writes)
        sem = self.sem[eng]
        self.cnt[eng] += 1
        val = self.cnt[eng]
        self.q[eng].append(lambda e, fn=fn, sem=sem: fn(e).then_inc(sem, 1))
        self._record((sem, eng, val, eng), reads, writes)

    def dma(self, qeng, out, in_, reads=(), writes=(), is_out=False):
        self._deps(qeng, reads, writes)
        r0 = writes[0] if writes else reads[0]
        if r0.dsem is None:
            if self.dsem_free:
                r0.dsem, r0.dkey, r0.dcnt = self.dsem_free.pop()
            else:
                self.ndsem += 1
                r0.dsem = self.stacks[0].enter_context(self.nc.semaphore(f"dsem{self.ndsem}"))
                r0.dkey = f"d{self.ndsem}"
                r0.dcnt = 0
            self.dsem_live[-1].append(r0)
        r0.dcnt += 16
        sem = r0.dsem
        tok = (sem, r0.dkey, r0.dcnt, "dma")
        self.q[qeng].append(
            lambda e, out=out, in_=in_, sem=sem: e.dma_start(out=out, in_=in_).then_inc(sem, 16)
        )
        self._record(tok, reads, writes)
        if is_out:
            self.out_toks.append(tok)
        return tok

    def finish(self):
        last = {}
        for (sem, key, val, _) in self.out_toks:
            if key not in last or last[key][1] < val:
                last[key] = (sem, val)
        for key, (sem, val) in last.items():
            self.q["sp"].append(lambda e, sem=sem, val=val: e.wait_ge(sem, val))
        with self.nc.Block() as block:
            @block.tensor
            def _(e):
                for f in self.q["pe"]:
                    f(e)

            @block.scalar
            def _(e):
                for f in self.q["act"]:
                    f(e)

            @block.vector
            def _(e):
                for f in self.q["dve"]:
                    f(e)

            @block.gpsimd
            def _(e):
                for f in self.q["pool"]:
                    f(e)

            @block.sync
            def _(e):
                for f in self.q["sp"]:
                    f(e)
        while self.stacks:
            self.stacks.pop().close()


class Ctx:
    pass


def din(nc, name, shape, dt):
    return nc.dram_tensor(name, list(shape), dt, kind="ExternalInput").ap()


def dout(nc, name, shape, dt):
    return nc.dram_tensor(name, list(shape), dt, kind="ExternalOutput").ap()


def tsl(tb):
    return slice(tb * TB, (tb + 1) * TB)


def tok_common(P, C):
    C.x = P.sb("x", [128, 8, NT], F32)
    C.Rx = [P.res("x") for _ in range(NB)]
    C.ones = P.sb("ones", [128, 128], BF16)
    C.Rones = P.res("ones")
    P.op("pool", lambda e: e.memset(C.ones[:], 1.0), writes=[C.Rones])
    C.sq = P.sb("sq", [128, 8, TB], BF16)
    C.Rsq = P.res("sq")
    C.ss = P.ps("ss", [128, TB])
    C.Rss = P.res("ss")
    C.lnv = P.sb("lnv", [128, TB], F32)
    C.Rlnv = P.res("lnv")
    C.rstd = P.sb("rstd", [128, TB], F32)
    C.Rrstd = P.res("rstd")
    C.xn = [P.sb(f"xn{i}", [128, 8, TB], BF16) for i in range(2)]
    C.Rxn = [P.res("xn") for _ in range(2)]
    C.nq = {"sp": 0}


def load_x(P, C, xT):
    xv = xT.rearrange("(c p) t -> p c t", p=128)
    for tb in range(NB):
        P.dma("sp", C.x[:, :, tsl(tb)], xv[:, :, tsl(tb)], writes=[C.Rx[tb]])


def store_x(P, C, xo):
    xv = xo.rearrange("(c p) t -> p c t", p=128)
    for tb in range(NB):
        P.dma("sp", xv[:, :, tsl(tb)], C.x[:, :, tsl(tb)], reads=[C.Rx[tb]], is_out=True)


def load_small(P, name, src, shape, dt=F32):
    t = P.sb(name, list(shape), dt)
    r = P.res(name)
    P.dma("sp", t[:], src, writes=[r])
    return t, r


def load_w_bf16(P, name, w_ap, kin, m, col_slices=None):
    t = P.sb(name, [128, kin, m], BF16)
    r = P.res(name)
    wv = w_ap.rearrange("(c p) f -> p c f", p=128)
    if col_slices is None:
        col_slices = [(0, 0, m)]
    for (d0, s0, n) in col_slices:
        for c in range(kin):
            P.dma("pool", t[:, c, d0:d0 + n], wv[:, c, s0:s0 + n], writes=[r])
    return t, r


def rms_fm(P, C, srcs, Rsrc, gain, Rgain, D, dsts, Rdst, rows=128, src_all=None):
    n = len(srcs)
    if src_all is not None:
        P.op("act", lambda e: e.activation(out=C.sq[:, 0:n, :], in_=src_all, func=AF.Square),
             reads=[Rsrc], writes=[C.Rsq])
    else:
        for c in range(n):
            P.op("act", lambda e, c=c: e.activation(out=C.sq[:rows, c, :], in_=srcs[c], func=AF.Square),
                 reads=[Rsrc], writes=[C.Rsq])

    def mm(e):
        ins = None
        for c in range(n):
            ins = e.matmul(C.ss[:], C.ones[:rows, :], C.sq[:rows, c, :], start=(c == 0), stop=(c == n - 1))
        return ins
    P.op("pe", mm, reads=[C.Rsq, C.Rones], writes=[C.Rss])
    P.op("act", lambda e: e.activation(out=C.lnv[:], in_=C.ss[:], func=AF.Ln, bias=EPS, scale=1.0 / D),
         reads=[C.Rss], writes=[C.Rlnv])
    P.op("act", lambda e: e.activation(out=C.rstd[:], in_=C.lnv[:], func=AF.Exp, scale=-0.5),
         reads=[C.Rlnv], writes=[C.Rrstd])
    for c in range(n):
        P.op("dve", lambda e, c=c: e.scalar_tensor_tensor(
            out=dsts[c], in0=srcs[c], scalar=gain[:rows, c:c + 1], in1=C.rstd[:rows, :],
            op0=ALU.mult, op1=ALU.mult), reads=[Rsrc, C.Rrstd, Rgain], writes=[Rdst])


def norm_x(P, C, tb, gain, Rgain, buf):
    srcs = [C.x[:, c, tsl(tb)] for c in range(8)]
    dsts = [C.xn[buf][:, c, :] for c in range(8)]
    rms_fm(P, C, srcs, C.Rx[tb], gain, Rgain, float(DM), dsts, C.Rxn[buf], src_all=C.x[:, :, tsl(tb)])


def lin_fm(e, out_ps, w, col0, ncols, rhs_list):
    ins = None
    n = len(rhs_list)
    for c in range(n):
        ins = e.matmul(out_ps, w[:, c, col0:col0 + ncols], rhs_list[c], start=(c == 0), stop=(c == n - 1))
    return ins


def rope_tables(P, C, pos_ap, invf_ap, sgn_ap):
    N = NT
    posi = P.sb("posi", [64, N], I32)
    posf = P.sb("posf", [64, N], F32)
    ang = P.sb("ang", [64, N], F32)
    kf = P.sb("kf", [64, N], F32)
    r = P.sb("rr", [64, N], F32)
    C.cos2 = P.sb("cos2", [64, N], F32)
    C.sin2 = P.sb("sin2", [64, N], F32)
    C.Rtab = P.res("tab")
    R = {n: P.res(n) for n in ["posi", "posf", "ang", "kf", "r", "c", "s"]}
    invf, Rinvf = load_small(P, "invf", invf_ap, [64, 1])
    sgn, Rsgn = load_small(P, "sgn", sgn_ap, [64, 1])
    P.dma("sp", posi[:], pos_ap.partition_broadcast(64), writes=[R["posi"]])
    P.op("dve", lambda e: e.tensor_copy(out=posf[:], in_=posi[:]), reads=[R["posi"]], writes=[R["posf"]])
    P.op("dve", lambda e: e.tensor_scalar(out=ang[:], in0=posf[:], scalar1=invf[:, 0:1], scalar2=None, op0=ALU.mult),
         reads=[R["posf"], Rinvf], writes=[R["ang"]])
    P.op("dve", lambda e: e.tensor_scalar(out=kf[:], in0=ang[:], scalar1=float(1 / (2 * math.pi)), scalar2=None,
                                          op0=ALU.mult), reads=[R["ang"]], writes=[R["kf"]])
    ki = posi
    P.op("dve", lambda e: e.tensor_copy(out=ki[:], in_=kf[:]), reads=[R["kf"]], writes=[R["posi"]])
    P.op("dve", lambda e: e.tensor_copy(out=kf[:], in_=ki[:]), reads=[R["posi"]], writes=[R["kf"]])
    C1 = 6.28125
    C2 = float(2 * math.pi - 6.28125)
    P.op("dve", lambda e: e.scalar_tensor_tensor(out=r[:], in0=kf[:], scalar=-C1, in1=ang[:], op0=ALU.mult, op1=ALU.add),
         reads=[R["ang"], R["kf"]], writes=[R["r"]])
    P.op("dve", lambda e: e.scalar_tensor_tensor(out=r[:], in0=kf[:], scalar=-C2, in1=r[:], op0=ALU.mult, op1=ALU.add),
         reads=[R["kf"], R["r"]], writes=[R["r"]])

    def wrap(dst, rd, shift):
        P.op("dve", lambda e: e.tensor_scalar(out=dst[:], in0=r[:], scalar1=float(shift), scalar2=None, op0=ALU.add),
             reads=[R["r"]], writes=[rd])
        P.op("dve", lambda e: e.tensor_scalar(out=ang[:], in0=dst[:], scalar1=math.pi, scalar2=-2 * math.pi,
                                              op0=ALU.is_gt, op1=ALU.mult), reads=[rd], writes=[R["ang"]])
        P.op("dve", lambda e: e.tensor_tensor(out=dst[:], in0=dst[:], in1=ang[:], op=ALU.add),
             reads=[rd, R["ang"]], writes=[rd])
        P.op("dve", lambda e: e.tensor_scalar(out=ang[:], in0=dst[:], scalar1=-math.pi, scalar2=2 * math.pi,
                                              op0=ALU.is_lt, op1=ALU.mult), reads=[rd], writes=[R["ang"]])
        P.op("dve", lambda e: e.tensor_tensor(out=dst[:], in0=dst[:], in1=ang[:], op=ALU.add),
             reads=[rd, R["ang"]], writes=[rd])
    wrap(posf, R["c"], math.pi / 2)
    wrap(kf, R["s"], 0.0)
    SC = 0.9999995
    P.op("act", lambda e: e.activation(out=C.cos2[:], in_=posf[:], func=AF.Sin, scale=SC),
         reads=[R["c"]], writes=[C.Rtab])
    P.op("dve", lambda e: e.tensor_scalar(out=kf[:], in0=kf[:], scalar1=sgn[:, 0:1], scalar2=None, op0=ALU.mult),
         reads=[R["s"], Rsgn], writes=[R["s"]])
    P.op("act", lambda e: e.activation(out=C.sin2[:], in_=kf[:], func=AF.Sin, scale=SC),
         reads=[R["s"]], writes=[C.Rtab])


def mla_pre(P, C, A):
    gmix, Rgmix = load_small(P, "gmix_p", A["gmix"], [128, 8])
    qg, Rqg = load_small(P, "qg", A["qg"], [128, 3])
    kvg, Rkvg = load_small(P, "kvg", A["kvg"], [128, 2])
    w, Rw = load_w_bf16(P, "w_in", A["w_in"], 8, 768,
                        [(0, 0, 704), (704, 672, 32), (736, 640, 32)])
    rope_tables(P, C, A["pos"], A["invf"], A["sgn"])
    P.dma("sp", A["tabs"][0:64, :], C.cos2[:], reads=[C.Rtab], is_out=True)
    P.dma("sp", A["tabs"][64:128, :], C.sin2[:], reads=[C.Rtab], is_out=True)
    hps = [P.ps(f"hps{i}", [128, TB]) for i in range(3)]
    Rhps = P.res("hps")
    lat_sb = [P.sb(f"lat{i}", [128, 3, TB], BF16) for i in range(2)]
    Rlat = [P.res("lat") for _ in range(2)]
    t1 = P.sb("rt1", [64, TB], F32)
    t2 = P.sb("rt2", [64, TB], F32)
    Rt1, Rt2 = P.res("t1"), P.res("t2")
    kro = P.sb("kro", [64, TB], BF16)
    Rkro = P.res("kro")
    latv = A["lat"]
    for tb in range(NB):
        b = tb % 2
        norm_x(P, C, tb, gmix, Rgmix, b)
        xn = [C.xn[b][:, c, :] for c in range(8)]
        for grp, (c0, nch, gain, Rg, D, row0) in enumerate(
                [(0, 3, qg, Rqg, float(QL), 0), (384, 2, kvg, Rkvg, float(KVL), 384)]):
            def mm(e, c0=c0, nch=nch, xn=xn):
                ins = None
                for i in range(nch):
                    ins = lin_fm(e, hps[i][:], w, c0 + i * 128, 128, xn)
                return ins
            P.op("pe", mm, reads=[C.Rxn[b], Rw], writes=[Rhps])
            lb = (tb * 2 + grp) % 2
            rms_fm(P, C, [hps[i][:] for i in range(nch)], Rhps, gain, Rg, D,
                   [lat_sb[lb][:, i, :] for i in range(nch)], Rlat[lb])
            for i in range(nch):
                P.dma("sp", latv[row0 + i * 128:row0 + (i + 1) * 128, tsl(tb)], lat_sb[lb][:, i, :],
                      reads=[Rlat[lb]], is_out=True)
        def mmr(e, xn=xn):
            lin_fm(e, hps[0][0:64, :], w, 640, 64, xn)
            return lin_fm(e, hps[1][0:64, :], w, 704, 64, xn)
        P.op("pe", mmr, reads=[C.Rxn[b], Rw], writes=[Rhps])
        P.op("dve", lambda e, tb=tb: e.tensor_tensor(out=t1[:], in0=hps[0][0:64, :], in1=C.cos2[:, tsl(tb)], op=ALU.mult),
             reads=[Rhps, C.Rtab], writes=[Rt1])
        P.op("dve", lambda e, tb=tb: e.tensor_tensor(out=t2[:], in0=hps[1][0:64, :], in1=C.sin2[:, tsl(tb)], op=ALU.mult),
             reads=[Rhps, C.Rtab], writes=[Rt2])
        P.op("pool", lambda e: e.tensor_tensor(out=kro[:], in0=t1[:], in1=t2[:], op=ALU.add),
             reads=[Rt1, Rt2], writes=[Rkro])
        P.dma("sp", latv[640:704, tsl(tb)], kro[:], reads=[Rkro], is_out=True)


def post_attn(P, C, A):
    wo, Rwo = load_w_bf16(P, "w_o", A["w_o"], 8, DM)
    ob = [P.sb(f"ob{i}", [128, 8, TB], BF16) for i in range(2)]
    Rob = [P.res("ob") for _ in range(2)]
    yps = [P.ps(f"yps{i}", [128, TB]) for i in range(2)]
    Ryps = [P.res("yps") for _ in range(2)]
    ov = A["oT"].rearrange("(c p) t -> p c t", p=128)
    for tb in range(NB):
        b = tb % 2
        P.dma("sp", ob[b][:], ov[:, :, tsl(tb)], writes=[Rob[b]])
        for oc in range(8):
            pb = oc % 2
            P.op("pe", lambda e, oc=oc, pb=pb, b=b: lin_fm(e, yps[pb][:], wo, oc * 128, 128,
                                                           [ob[b][:, c, :] for c in range(8)]),
                 reads=[Rob[b], Rwo], writes=[Ryps[pb]])
            P.op("dve", lambda e, oc=oc, pb=pb, tb=tb: e.tensor_tensor(
                out=C.x[:, oc, tsl(tb)], in0=C.x[:, oc, tsl(tb)], in1=yps[pb][:], op=ALU.add),
                reads=[C.Rx[tb], Ryps[pb]], writes=[C.Rx[tb]])


def mlp(P, C, A):
    gain, Rgain = load_small(P, "gmlp", A["gmlp"], [128, 8])
    TH = 1024
    NHB = TH // TB
    HH = 16
    xnh = P.sb("xnh", [128, 8, TH], BF16)
    Rxnh = P.res("xnh")
    hid = P.sb("hid", [128, HH, TH], BF16)
    Rhid = [P.res("hid") for _ in range(HH)]
    NWU = 3
    wu = [P.sb(f"wu{i}", [128, 8, 512], BF16) for i in range(NWU)]
    Rwu = [P.res("wu") for _ in range(NWU)]
    NWD = 3
    wd = [P.sb(f"wd{i}", [128, HH, 128], BF16) for i in range(NWD)]
    Rwd = [P.res("wd") for _ in range(NWD)]
    hps = [P.ps(f"mhps{i}", [128, TB]) for i in range(3)]
    Rhps = [P.res("mhps") for _ in range(3)]
    yps = [P.ps(f"myps{i}", [128, TB]) for i in range(2)]
    Ryps = [P.res("myps") for _ in range(2)]
    hsq = [P.sb(f"hsq{i}", [128, TB], F32) for i in range(2)]
    Rhsq = [P.res("hsq") for _ in range(2)]
    wuv = A["w_up"].rearrange("(c p) f -> p c f", p=128)
    wdv = A["w_down"].rearrange("(c p) f -> p c f", p=128)
    iwu = 0
    iwd = 0
    ihp = 0
    iyp = 0
    ihs = 0
    for th in range(NT // TH):
        for sb_ in range(NHB):
            tb = th * NHB + sb_
            srcs = [C.x[:, c, tsl(tb)] for c in range(8)]
            dsts = [xnh[:, c, sb_ * TB:(sb_ + 1) * TB] for c in range(8)]
            rms_fm(P, C, srcs, C.Rx[tb], gain, Rgain, float(DM), dsts, Rxnh, src_all=C.x[:, :, tsl(tb)])
        for hh in range(2):
            for g in range(4):
                wb = iwu % NWU
                iwu += 1
                col0 = hh * 2048 + g * 512
                for c in range(8):
                    P.dma("pool", wu[wb][:, c, :], wuv[:, c, col0:col0 + 512], writes=[Rwu[wb]])
                for hc4 in range(4):
                    hc = g * 4 + hc4
                    for sb_ in range(NHB):
                        pb = ihp % 3
                        ihp += 1
                        P.op("pe", lambda e, pb=pb, wb=wb, hc4=hc4, sb_=sb_: lin_fm(
                            e, hps[pb][:], wu[wb], hc4 * 128, 128,
                            [xnh[:, c, sb_ * TB:(sb_ + 1) * TB] for c in range(8)]),
                            reads=[Rxnh, Rwu[wb]], writes=[Rhps[pb]])
                        sbuf = ihs % 2
                        ihs += 1
                        P.op("act", lambda e, pb=pb, sbuf=sbuf: e.activation(out=hsq[sbuf][:], in_=hps[pb][:], func=AF.Square),
                             reads=[Rhps[pb]], writes=[Rhsq[sbuf]])
                        P.op("dve", lambda e, pb=pb, sbuf=sbuf, hc=hc, sb_=sb_: e.scalar_tensor_tensor(
                            out=hid[:, hc, sb_ * TB:(sb_ + 1) * TB], in0=hps[pb][:], scalar=0.0, in1=hsq[sbuf][:],
                            op0=ALU.is_gt, op1=ALU.mult), reads=[Rhps[pb], Rhsq[sbuf]], writes=[Rhid[hc]])
            for oc in range(8):
                wb = iwd % NWD
                iwd += 1
                for q4 in range(4):
                    P.dma("pool", wd[wb][:, q4 * 4:(q4 + 1) * 4, :],
                          wdv[:, hh * HH + q4 * 4: hh * HH + (q4 + 1) * 4, oc * 128:(oc + 1) * 128], writes=[Rwd[wb]])
                for sb_ in range(NHB):
                    tb = th * NHB + sb_
                    pb = iyp % 2
                    iyp += 1

                    def mm(e, pb=pb, wb=wb, sb_=sb_):
                        ins = None
                        for hc in range(HH):
                            ins = e.matmul(yps[pb][:], wd[wb][:, hc, :], hid[:, hc, sb_ * TB:(sb_ + 1) * TB],
                                           start=(hc == 0), stop=(hc == HH - 1))
                        return ins
                    P.op("pe", mm, reads=Rhid + [Rwd[wb]], writes=[Ryps[pb]])
                    P.op("dve", lambda e, oc=oc, pb=pb, tb=tb: e.tensor_tensor(
                        out=C.x[:, oc, tsl(tb)], in0=C.x[:, oc, tsl(tb)], in1=yps[pb][:], op=ALU.add),
                        reads=[C.Rx[tb], Ryps[pb]], writes=[C.Rx[tb]])


def final_norm(P, C, A):
    gain, Rgain = load_small(P, "gfin", A["gfin"], [128, 8])
    ob = [P.sb(f"fo{i}", [128, 8, TB], F32) for i in range(2)]
    Rob = [P.res("fo") for _ in range(2)]
    ov = A["out"].rearrange("(c p) t -> p c t", p=128)
    for tb in range(NB):
        b = tb % 2
        srcs = [C.x[:, c, tsl(tb)] for c in range(8)]
        dsts = [ob[b][:, c, :] for c in range(8)]
        rms_fm(P, C, srcs, C.Rx[tb], gain, Rgain, float(DM), dsts, Rob[b], src_all=C.x[:, :, tsl(tb)])
        P.dma("sp", ov[:, :, tsl(tb)], ob[b][:], reads=[Rob[b]], is_out=True)


NQB = S // TB
NKT = S // 128


def build_attn():
    nc = bass.Bass("TRN2", target_bir_lowering=False)
    latq = din(nc, "latq", [QL, S], BF16)
    latkv = din(nc, "latkv", [KVL, S], BF16)
    latr = din(nc, "latr", [ROPE, S], BF16)
    cosd = din(nc, "cos2", [64, S], F32)
    sind = din(nc, "sin2", [64, S], F32)
    wq_d = din(nc, "wq", [QL, 256], F32)
    wkv_d = din(nc, "wkv", [KVL, 256], F32)
    oT = dout(nc, "oT", [VH, S], BF16)
    P = Prog(nc)
    KT = P.sb("KT", [128, S], BF16)
    KR = P.sb("KR", [65, S], BF16)
    V = P.sb("V", [128, NKT, 128], BF16)
    RK = [P.res("K") for _ in range(NQB)]
    ones = P.sb("ones", [128, 128], BF16)
    Rones = P.res("ones")
    P.op("pool", lambda e: e.memset(ones[:], 1.0), writes=[Rones])
    tri = P.sb("tri", [128, 128], BF16)
    Rtri = P.res("tri")
    P.op("pool", lambda e: e.memset(tri[:], 1.0), writes=[Rtri])
    P.op("pool", lambda e: e.affine_select(out=tri[:], in_=tri[:], pattern=[[1, 128]], compare_op=ALU.is_ge,
                                           fill=0.0, base=0, channel_multiplier=-1), reads=[Rtri], writes=[Rtri])
    RKR64 = P.res("kr64")
    P.op("pool", lambda e: e.memset(KR[64:65, :], 1.0), writes=[RKR64])
    wq, Rwq = load_w_bf16(P, "wq", wq_d, 3, 256)
    wkv, Rwkv = load_w_bf16(P, "wkv", wkv_d, 2, 256)
    kmx = P.sb("kmx", [128, TB], F32)
    Rkmx = P.res("kmx")
    P.op("dve", lambda e: e.memset(kmx[:], 0.0), writes=[Rkmx])
    kmax2 = P.sb("kmax2", [128, 1], F32)
    Rkmax2 = P.res("kmax2")

    st = [P.ps(f"st{i}", [128, TB]) for i in range(2)]
    Rst = [P.res("st") for _ in range(2)]
    oacc = [P.ps(f"oacc{i}", [128, TB]) for i in range(2)]
    Roacc = [P.res("oacc") for _ in range(2)]
    lacc = [P.ps(f"lacc{i}", [128, TB]) for i in range(2)]
    Rlacc = [P.res("lacc") for _ in range(2)]
    pa = P.ps("pa", [128, TB])
    Rpa = P.res("pa")
    pb = P.ps("pb", [128, TB])
    Rpb = P.res("pb")

    lkv = [P.sb(f"lkv{i}", [128, 2, TB], BF16) for i in range(2)]
    Rlkv = [P.res("lkv") for _ in range(2)]
    sqk = P.sb("sqk", [128, TB], BF16)
    sqr = P.sb("sqr", [64, TB], BF16)
    Rsqk = P.res("sqk")
    lkvv = latkv.rearrange("(c p) t -> p c t", p=128)
    for tb in range(NQB):
        b = tb % 2
        P.dma("sp", lkv[b][:], lkvv[:, :, tsl(tb)], writes=[Rlkv[b]])
        P.dma("sp", KR[0:64, tsl(tb)], latr[:, tsl(tb)], writes=[RK[tb]])
        P.op("pe", lambda e, b=b: lin_fm(e, pa[:], wkv, 0, 128, [lkv[b][:, c, :] for c in range(2)]),
             reads=[Rlkv[b], Rwkv], writes=[Rpa])
        P.op("act", lambda e, tb=tb: e.activation(out=KT[:, tsl(tb)], in_=pa[:], func=AF.Copy),
             reads=[Rpa], writes=[RK[tb]])
        P.op("act", lambda e: e.activation(out=sqk[:], in_=pa[:], func=AF.Square), reads=[Rpa], writes=[Rsqk])
        P.op("act", lambda e, tb=tb: e.activation(out=sqr[:], in_=KR[0:64, tsl(tb)], func=AF.Square),
             reads=[RK[tb]], writes=[Rsqk])

        def mmv(e, b=b):
            ins = None
            for tt in range(4):
                for c in range(2):
                    ins = e.matmul(pb[:, tt * 128:(tt + 1) * 128], lkv[b][:, c, tt * 128:(tt + 1) * 128],
                                   wkv[:, c, 128:256], start=(c == 0), stop=(c == 1))
            return ins
        P.op("pe", mmv, reads=[Rlkv[b], Rwkv], writes=[Rpb])
        P.op("dve", lambda e, tb=tb: e.tensor_copy(
            out=V[:, tb * 4:(tb + 1) * 4, :], in_=pb[:].rearrange("p (a b) -> p a b", b=128)),
            reads=[Rpb], writes=[RK[tb]])

        def mmn(e):
            e.matmul(pa[:], ones[:, :], sqk[:], start=True, stop=False)
            return e.matmul(pa[:], ones[0:64, :], sqr[:], start=False, stop=True)
        P.op("pe", mmn, reads=[Rsqk, Rones], writes=[Rpa])
        P.op("dve", lambda e: e.tensor_tensor(out=kmx[:], in0=kmx[:], in1=pa[:], op=ALU.max),
             reads=[Rkmx, Rpa], writes=[Rkmx])
    P.op("dve", lambda e: e.reduce_max(out=kmax2[:], in_=kmx[:], axis=mybir.AxisListType.X),
         reads=[Rkmx], writes=[Rkmax2])

    lq = [P.sb(f"lq{i}", [128, 3, TB], BF16) for i in range(2)]
    Rlq = [P.res("lq") for _ in range(2)]
    cs = [P.sb(f"cs{i}", [64, TB], F32) for i in range(2)]
    sn = [P.sb(f"sn{i}", [64, TB], F32) for i in range(2)]
    Rcs = [P.res("cs") for _ in range(2)]
    qn = [P.sb(f"qn{i}", [128, TB], BF16) for i in range(2)]
    qr = [P.sb(f"qr{i}", [65, TB], BF16) for i in range(2)]
    Rq = [P.res("q") for _ in range(2)]
    qrf = P.sb("qrf", [64, TB], F32)
    qt1 = P.sb("qt1", [64, TB], F32)
    qt2 = P.sb("qt2", [64, TB], F32)
    Rqrf, Rqt1, Rqt2 = P.res("qrf"), P.res("qt1"), P.res("qt2")
    sq1 = P.sb("sq1", [128, TB], BF16)
    sq2 = P.sb("sq2", [64, TB], BF16)
    Rsq1 = P.res("sq1")
    qnrm = P.sb("qnrm", [128, TB], F32)
    Rqnrm = P.res("qnrm")
    NPT = 4
    pt = [P.sb(f"pt{i}", [128, TB], BF16) for i in range(NPT)]
    Rpt = [P.res("pt") for _ in range(NPT)]
    rl = P.sb("rl", [128, TB], F32)
    Rrl = P.res("rl")
    obf = [P.sb(f"obf{i}", [128, TB], BF16) for i in range(2)]
    Robf = [P.res("obf") for _ in range(2)]
    lqv = latq.rearrange("(c p) t -> p c t", p=128)

    def load_q(j):
        b = j % 2
        P.dma("sp", lq[b][:], lqv[:, :, tsl(j)], writes=[Rlq[b]])
        P.dma("sp", cs[b][:], cosd[:, tsl(j)], writes=[Rcs[b]])
        P.dma("sp", sn[b][:], sind[:, tsl(j)], writes=[Rcs[b]])

    def proj_q(j):
        b = j % 2
        rhs = [lq[b][:, c, :] for c in range(3)]
        P.op("pe", lambda e: lin_fm(e, pa[:], wq, 0, 128, rhs), reads=[Rlq[b], Rwq], writes=[Rpa])
        P.op("act", lambda e: e.activation(out=qn[b][:], in_=pa[:], func=AF.Copy), reads=[Rpa], writes=[Rq[b]])
        P.op("act", lambda e: e.activation(out=sq1[:], in_=pa[:], func=AF.Square), reads=[Rpa], writes=[Rsq1])

        def mmr(e):
            lin_fm(e, pb[0:64, :], wq, 128, 64, rhs)
            return lin_fm(e, pb[64:128, :], wq, 192, 64, rhs)
        pbs = P.ps
        P.op("pe", lambda e: lin_fm(e, pb[0:64, :], wq, 128, 64, rhs), reads=[Rlq[b], Rwq], writes=[Rpb])
        P.op("dve", lambda e: e.tensor_tensor(out=qt1[:], in0=pb[0:64, :], in1=cs[b][:], op=ALU.mult),
             reads=[Rpb, Rcs[b]], writes=[Rqt1])
        P.op("pe", lambda e: lin_fm(e, pb[0:64, :], wq, 192, 64, rhs), reads=[Rlq[b], Rwq], writes=[Rpb])
        P.op("dve", lambda e: e.tensor_tensor(out=qt2[:], in0=pb[0:64, :], in1=sn[b][:], op=ALU.mult),
             reads=[Rpb, Rcs[b]], writes=[Rqt2])
        P.op("pool", lambda e: e.tensor_tensor(out=qrf[:], in0=qt1[:], in1=qt2[:], op=ALU.add),
             reads=[Rqt1, Rqt2], writes=[Rqrf])
        P.op("act", lambda e: e.activation(out=qr[b][0:64, :], in_=qrf[:], func=AF.Copy), reads=[Rqrf], writes=[Rq[b]])
        P.op("act", lambda e: e.activation(out=sq2[:], in_=qrf[:], func=AF.Square), reads=[Rqrf], writes=[Rsq1])

        def mmn(e):
            e.matmul(pa[:], ones[:, :], sq1[:], start=True, stop=False)
            return e.matmul(pa[:], ones[0:64, :], sq2[:], start=False, stop=True)
        P.op("pe", mmn, reads=[Rsq1, Rones], writes=[Rpa])
        P.op("act", lambda e: e.activation(out=qnrm[:], in_=pa[:], func=AF.Sqrt, scale=kmax2[:, 0:1]),
             reads=[Rpa, Rkmax2], writes=[Rqnrm])
        P.op("dve", lambda e: e.tensor_scalar(out=qr[b][64:65, :], in0=qnrm[64:65, :], scalar1=-1.0, scalar2=None,
                                              op0=ALU.mult), reads=[Rqnrm], writes=[Rq[b]])

    units = []
    for j in range(NQB):
        for kt in range(4 * j + 4):
            units.append((j, kt))

    def c0_of(j, kt):
        i = kt - 4 * j
        return 128 * i if i > 0 else 0

    def emit_qk(ui):
        j, kt = units[ui]
        b = j % 2
        sb_ = ui % 2
        c0 = c0_of(j, kt)

        def mm(e):
            e.matmul(st[sb_][:, c0:], KT[:, kt * 128:(kt + 1) * 128], qn[b][:, c0:], start=True, stop=False)
            return e.matmul(st[sb_][:, c0:], KR[0:65, kt * 128:(kt + 1) * 128], qr[b][0:65, c0:], start=False, stop=True)
        P.op("pe", mm, reads=[RK[kt // 4], RKR64, Rq[b]], writes=[Rst[sb_]])

    def emit_exp(ui):
        j, kt = units[ui]
        sb_ = ui % 2
        pi = ui % NPT
        c0 = c0_of(j, kt)
        P.op("act", lambda e: e.activation(out=pt[pi][:, c0:], in_=st[sb_][:, c0:], func=AF.Exp, scale=ATT_SCALE),
             reads=[Rst[sb_]], writes=[Rpt[pi]])
        if kt >= 4 * j:
            P.op("dve", lambda e: e.tensor_tensor(out=pt[pi][:, c0:c0 + 128], in0=pt[pi][:, c0:c0 + 128], in1=tri[:],
                                                  op=ALU.mult), reads=[Rpt[pi], Rtri], writes=[Rpt[pi]])

    def emit_pv(ui):
        j, kt = units[ui]
        ab = j % 2
        pi = ui % NPT
        c0 = c0_of(j, kt)
        last = (kt == 4 * j + 3)

        def mm(e):
            e.matmul(oacc[ab][:, c0:], V[:, kt, :], pt[pi][:, c0:], start=(kt == 0), stop=last)
            return e.matmul(lacc[ab][:, c0:], ones[:, :], pt[pi][:, c0:], start=(kt == 0), stop=last)
        P.op("pe", mm, reads=[RK[kt // 4], Rpt[pi], Rones], writes=[Roacc[ab], Rlacc[ab]])

    def finalize(j):
        ab = j % 2
        P.op("dve", lambda e: e.reciprocal(out=rl[:], in_=lacc[ab][:]), reads=[Rlacc[ab]], writes=[Rrl])
        P.op("dve", lambda e: e.tensor_tensor(out=obf[ab][:], in0=oacc[ab][:], in1=rl[:], op=ALU.mult),
             reads=[Roacc[ab], Rrl], writes=[Robf[ab]])
        P.dma("sp", oT[:, tsl(j)], obf[ab][:], reads=[Robf[ab]], is_out=True)

    load_q(0)
    load_q(1)
    proj_q(0)
    emit_qk(0)
    for ui, (j, kt) in enumerate(units):
        nk = 4 * j + 4
        emit_exp(ui)
        if kt == nk // 2 and j + 1 < NQB:
            proj_q(j + 1)
            if j + 2 < NQB:
                load_q(j + 2)
        if ui + 1 < len(units):
            emit_qk(ui + 1)
        emit_pv(ui)
        if kt == nk - 1:
            finalize(j)
    P.finish()
    return nc


GLA_DBG = 0


def gla_a(P, C, A):
    gmix, Rgmix = load_small(P, "gmix_g", A["gmix"], [128, 8])
    w, Rw = load_w_bf16(P, "gw_in", A["gw_in"], 8, 2064, [(0, 0, 2048), (2048, 3072, 16)])
    wgk = P.sb("wgk", [16, GDK], BF16)
    Rwgk = P.res("wgk")
    P.dma("pool", wgk[:], A["w_gk"], writes=[Rwgk])
    bgk, Rbgk = load_small(P, "bgk", A["b_gk"], [128, 4])
    negb = P.sb("negb", [128, 4], F32)
    Rnegb = P.res("negb")
    P.op("dve", lambda e: e.tensor_scalar(out=negb[:], in0=bgk[:], scalar1=-1.0, scalar2=None, op0=ALU.mult),
         reads=[Rbgk], writes=[Rnegb])
    ident = P.sb("ident", [128, 128], BF16)
    Rident = P.res("ident")
    P.op("pool", lambda e: e.memset(ident[:], 1.0), writes=[Rident])
    P.op("pool", lambda e: e.affine_select(out=ident[:], in_=ident[:], pattern=[[1, 128]], compare_op=ALU.is_equal,
                                           fill=0.0, base=0, channel_multiplier=-1), reads=[Rident], writes=[Rident])
    mask2 = P.sb("mask2", [128, 128], BF16)
    Rmask2 = P.res("mask2")
    P.op("pool", lambda e: e.memset(mask2[:], 1.0), writes=[Rmask2])
    P.op("pool", lambda e: e.affine_select(out=mask2[:], in_=mask2[:], pattern=[[1, 128]], compare_op=ALU.is_ge,
                                           fill=0.0, base=0, channel_multiplier=-1), reads=[Rmask2], writes=[Rmask2])
    P.op("pool", lambda e: e.memset(mask2[0:64, 64:128], 0.0), reads=[Rmask2], writes=[Rmask2])
    rmask = P.sb("rmask", [128, TB], F32)
    onesf = P.sb("onesf", [128, TB], F32)
    Rrm = P.res("rmask")
    P.op("pool", lambda e: e.memset(rmask[:], 1.0), writes=[Rrm])
    P.op("pool", lambda e: e.memset(rmask[:].rearrange("p (c s) -> p c s", s=CH)[:, :, 0:1], 0.0),
         reads=[Rrm], writes=[Rrm])
    P.op("pool", lambda e: e.memset(onesf[:], 1.0), reads=[Rrm], writes=[Rrm])
    Sf = P.sb("Sf", [128, 4, GHV], F32)
    RSf = [P.res("Sf") for _ in range(4)]
    Sbf = [P.sb(f"Sbf{i}", [128, 4, GHV], BF16) for i in range(2)]
    RSbf = [[P.res("Sbf") for _ in range(4)] for _ in range(2)]
    for h in range(4):
        P.op("dve", lambda e, h=h: e.memset(Sf[:, h, :], 0.0), writes=[RSf[h]])
        P.op("pool", lambda e, h=h: e.memset(Sbf[0][:, h, :], 0.0), writes=[RSbf[0][h]])
    Glast = P.sb("Glast", [128, 4], F32)
    RGlast = P.res("Glast")
    P.op("dve", lambda e: e.memset(Glast[:], 0.0), writes=[RGlast])
    a_sb = P.sb("a_sb", [16, TB], BF16)
    Ra = P.res("a_sb")
    e1 = P.sb("e1", [128, TB], F32)
    Re1 = P.res("e1")
    lt = P.sb("lt", [128, TB], F32)
    Rlt = P.res("lt")
    Lc = P.sb("Lc", [128, TB], F32)
    RLc = P.res("Lc")
    G = P.sb("G", [128, TB], F32)
    RG = P.res("G")
    EA = [P.sb(f"EA{i}", [128, TB], F32) for i in range(2)]
    REA = [P.res("EA") for _ in range(2)]
    qt = P.sb("qt", [128, 4, TB], BF16)
    kt_ = P.sb("kt_", [128, 4, TB], BF16)
    kh = P.sb("kh", [128, 4, TB], BF16)
    Rqt = [P.res("qt") for _ in range(4)]
    Rkt = [P.res("kt") for _ in range(4)]
    Rkh = [P.res("kh") for _ in range(4)]
    qh = [P.sb(f"qh{i}", [128, TB], BF16) for i in range(2)]
    Rqh = [P.res("qh") for _ in range(2)]
    nLl = P.sb("nLl", [128, 4, 8], F32)
    dch = P.sb("dch", [128, 4, 8], F32)
    RnLl = [P.res("nLl") for _ in range(4)]
    Rdch = [P.res("dch") for _ in range(4)]
    khat = P.sb("khat", [64, 4, 2, 4, 128], BF16)
    v_od = P.sb("v_od", [64, 4, GDV], BF16)
    Rkhat = [[P.res("khat") for _ in range(4)] for _ in range(4)]
    v_sb = P.sb("v_sb", [128, 4, GDV], BF16)
    Rv = [P.res("v") for _ in range(4)]
    AT = [P.sb(f"AT{i}", [128, 128], BF16) for i in range(2)]
    RAT = [P.res("AT") for _ in range(2)]
    ost = [P.sb(f"ost{i}", [128, 8, TB], BF16) for i in range(2)]
    Rost = [P.res("ost") for _ in range(2)]
    Dt = P.sb("Dt", [128, 4], F32)
    RDt = P.res("Dt")
    g_ps = P.ps("g_ps", [128, TB])
    Rg_ps = P.res("g_ps")
    q_ps = P.ps("q_ps", [128, TB])
    Rq_ps = P.res("q_ps")
    k_ps = P.ps("k_ps", [128, TB])
    Rk_ps = P.res("k_ps")
    v_ps = P.ps("v_ps", [128, TB])
    Rv_ps = P.res("v_ps")
    misc = P.ps("misc", [128, TB])
    Rat_ps = P.res("at_ps")
    ub = P.ps("ub", [128, TB])
    Rub = [P.res("ub"), Rv_ps]
    ubs = [ub[:, 0:256], v_ps[:, 0:256]]
    tr_ps = P.ps("tr_ps", [128, 128], BF16)
    Rtr = P.res("tr")
    at_ps = misc[:, 0:128]
    o_pss = [q_ps[:, 0:256], q_ps[:, 256:512], k_ps[:, 0:256], k_ps[:, 256:512]]
    Ro01, Ro23 = P.res("o_ps01"), P.res("o_ps23")
    Ro = [Ro01, Ro01, Ro23, Ro23]
    q3 = q_ps[:].rearrange("p (a b) -> p a b", b=128)
    k3 = k_ps[:].rearrange("p (a b) -> p a b", b=128)
    o_pss3 = [q3[:, 0:2, :], q3[:, 2:4, :], k3[:, 0:2, :], k3[:, 2:4, :]]
    qscale = float(GHK ** -0.5)
    olv = A["oloc"].rearrange("(c p) t -> p c t", p=128)
    iAT = 0
    iU = 0
    mchunk = 0
    for tb in range(NB):
        b = tb % 2
        norm_x(P, C, tb, gmix, Rgmix, b)
        xn = [C.xn[b][:, c, :] for c in range(8)]
        P.op("pe", lambda e, xn=xn: lin_fm(e, g_ps[0:16, :], w, 2048, 16, xn), reads=[C.Rxn[b], Rw], writes=[Rg_ps])
        P.op("act", lambda e: e.activation(out=a_sb[:], in_=g_ps[0:16, :], func=AF.Copy), reads=[Rg_ps], writes=[Ra])
        for p in range(4):
            for half in range(2):
                def mmv(e, p=p, half=half, b=b):
                    ins = None
                    for c in range(8):
                        ins = e.matmul(v_ps[:], C.xn[b][:, c, p * 128:(p + 1) * 128],
                                       w[:, c, 1024 + half * 512: 1024 + (half + 1) * 512],
                                       start=(c == 0), stop=(c == 7))
                    return ins
                P.op("pe", mmv, reads=[C.Rxn[b], Rw], writes=[Rv_ps])
                eng = "act" if half == 0 else "dve"
                if eng == "act":
                    P.op("act", lambda e, p=p, half=half: e.activation(
                        out=v_sb[:, p, half * 512:(half + 1) * 512], in_=v_ps[:], func=AF.Copy),
                        reads=[Rv_ps], writes=[Rv[p]])
                else:
                    P.op("dve", lambda e, p=p, half=half: e.tensor_copy(
                        out=v_sb[:, p, half * 512:(half + 1) * 512], in_=v_ps[:]),
                        reads=[Rv_ps], writes=[Rv[p]])

                def mmv2(e, p=p, half=half, b=b):
                    ins = None
                    for c in range(8):
                        ins = e.matmul(v_ps[0:64, :], C.xn[b][:, c, p * 128 + 64:(p + 1) * 128],
                                       w[:, c, 1024 + half * 512: 1024 + (half + 1) * 512],
                                       start=(c == 0), stop=(c == 7))
                    return ins
                P.op("pe", mmv2, reads=[C.Rxn[b], Rw], writes=[Rv_ps])
                P.op("dve", lambda e, p=p, half=half: e.tensor_copy(
                    out=v_od[:, p, half * 512:(half + 1) * 512], in_=v_ps[0:64, :]),
                    reads=[Rv_ps], writes=[Rv[p]])
                continue
                if eng == "act":
                    P.op("act", lambda e, p=p, half=half: e.activation(
                        out=v_sb[:, p, half * 512:(half + 1) * 512], in_=v_ps[:], func=AF.Copy),
                        reads=[Rv_ps], writes=[Rv[p]])
                else:
                    P.op("dve", lambda e, p=p, half=half: e.tensor_copy(
                        out=v_sb[:, p, half * 512:(half + 1) * 512], in_=v_ps[:]),
                        reads=[Rv_ps], writes=[Rv[p]])
        for h in range(4):
            P.op("pe", lambda e, h=h: e.matmul(g_ps[:], wgk[0:16, h * 128:(h + 1) * 128], a_sb[0:16, :],
                                               start=True, stop=True), reads=[Ra, Rwgk], writes=[Rg_ps])
            P.op("act", lambda e, h=h: e.activation(out=e1[:], in_=g_ps[:], func=AF.Exp, bias=negb[:, h:h + 1], scale=-1.0),
                 reads=[Rg_ps, Rnegb], writes=[Re1])
            P.op("act", lambda e: e.activation(out=lt[:], in_=e1[:], func=AF.Ln, bias=1.0, scale=1.0),
                 reads=[Re1], writes=[Rlt])
            P.op("dve", lambda e: e.tensor_tensor_scan(out=Lc[:], data0=rmask[:], data1=lt[:], initial=0.0,
                                                       op0=ALU.mult, op1=ALU.add), reads=[Rlt, Rrm], writes=[RLc])
            P.op("dve", lambda e, h=h: e.tensor_tensor_scan(out=G[:], data0=onesf[:], data1=lt[:],
                                                            initial=Glast[:, h:h + 1], op0=ALU.mult, op1=ALU.add),
                 reads=[Rlt, Rrm, RGlast], writes=[RG])
            P.op("dve", lambda e, h=h: e.tensor_copy(out=Glast[:, h:h + 1], in_=G[:, TB - 1:TB]),
                 reads=[RG], writes=[RGlast])
            P.op("pe", lambda e, h=h, xn=xn: lin_fm(e, q_ps[:], w, h * 128, 128, xn), reads=[C.Rxn[b], Rw],
                 writes=[Rq_ps, Ro[0], Ro[1]])
            P.op("pe", lambda e, h=h, xn=xn: lin_fm(e, k_ps[:], w, 512 + h * 128, 128, xn), reads=[C.Rxn[b], Rw],
                 writes=[Rk_ps, Ro[2], Ro[3]])
            P.op("act", lambda e: e.activation(out=EA[0][:], in_=Lc[:], func=AF.Exp, scale=-1.0 / 16),
                 reads=[RLc], writes=[REA[0]])
            P.op("dve", lambda e, h=h: e.scalar_tensor_tensor(out=qt[:, h, :], in0=q_ps[:], scalar=qscale, in1=EA[0][:],
                                                              op0=ALU.mult, op1=ALU.mult),
                 reads=[Rq_ps, REA[0]], writes=[Rqt[h]])
            P.op("act", lambda e: e.activation(out=EA[1][:], in_=Lc[:], func=AF.Exp, scale=1.0 / 16),
                 reads=[RLc], writes=[REA[1]])
            P.op("dve", lambda e, h=h: e.tensor_tensor(out=kt_[:, h, :], in0=k_ps[:], in1=EA[1][:], op=ALU.mult),
                 reads=[Rk_ps, REA[1]], writes=[Rkt[h]])
            P.op("dve", lambda e, h=h: e.tensor_scalar(
                out=nLl[:, h, :], in0=Lc[:].rearrange("p (c s) -> p c s", s=CH)[:, :, CH - 1],
                scalar1=-1.0 / 16, scalar2=None, op0=ALU.mult), reads=[RLc], writes=[RnLl[h]])
            P.op("act", lambda e, h=h: e.activation(out=dch[:, h, :], in_=nLl[:, h, :], func=AF.Exp),
                 reads=[RnLl[h]], writes=[Rdch[h]])
            for ci in range(8):
                P.op("act", lambda e, h=h, ci=ci: e.activation(
                    out=EA[0][:, ci * CH:(ci + 1) * CH], in_=Lc[:, ci * CH:(ci + 1) * CH], func=AF.Exp,
                    bias=nLl[:, h, ci:ci + 1], scale=1.0 / 16), reads=[RLc, RnLl[h]], writes=[REA[0]])
            P.op("dve", lambda e, h=h: e.tensor_tensor(out=kh[:, h, :], in0=k_ps[:], in1=EA[0][:], op=ALU.mult),
                 reads=[Rk_ps, REA[0]], writes=[Rkh[h]])
            P.op("act", lambda e: e.activation(out=EA[1][:], in_=G[:], func=AF.Exp, scale=-1.0 / 16),
                 reads=[RG], writes=[REA[1]])
            qb = (tb * 4 + h) % 2
            P.op("dve", lambda e, qb=qb: e.scalar_tensor_tensor(out=qh[qb][:], in0=q_ps[:], scalar=qscale, in1=EA[1][:],
                                                                op0=ALU.mult, op1=ALU.mult),
                 reads=[Rq_ps, REA[1]], writes=[Rqh[qb]])
            P.dma("sp", A["qhat"][h * 128:(h + 1) * 128, tsl(tb)], qh[qb][:], reads=[Rqh[qb]], is_out=True)
        ob_ = tb % 2
        for p in range(4 if GLA_DBG != 1 else 0):
            psl = slice(p * 128, (p + 1) * 128)
            for stage in range(2):
                for h in range(4):
                    m = mchunk + stage
                    cur = m % 2
                    do_o = GLA_DBG in (0, 3)
                    do_u = GLA_DBG in (0, 5)
                    if stage == 0:
                        for st_ in range(2):
                            c0_ = p * 128 + st_ * CH
                            P.op("pe", lambda e, h=h, c0_=c0_: e.transpose(tr_ps[0:64, :], kh[:, h, c0_:c0_ + CH], ident[:]),
                                 reads=[Rkh[h], Rident], writes=[Rtr])
                            P.op("act", lambda e, h=h, p=p, st_=st_: e.activation(
                                out=khat[:, p, st_, h, :], in_=tr_ps[0:64, :], func=AF.Copy),
                                reads=[Rtr], writes=[Rkhat[p][h]])
                    if stage == 0 and do_o:
                        P.op("pe", lambda e, h=h, psl=psl: e.matmul(at_ps, kt_[:, h, psl], qt[:, h, psl],
                                                                   start=True, stop=True),
                             reads=[Rkt[h], Rqt[h]], writes=[Rat_ps])
                        ia = iAT % 2
                        iAT += 1
                        P.op("dve", lambda e, ia=ia: e.tensor_tensor(out=AT[ia][:], in0=at_ps, in1=mask2[:], op=ALU.mult),
                             reads=[Rat_ps, Rmask2], writes=[RAT[ia]])

                        o_ps = o_pss[h]

                        def mmo(e, h=h, p=p, psl=psl, ia=ia, cur=cur, o_ps=o_ps):
                            for ec in range(2):
                                e.matmul(o_ps[:, ec * 128:(ec + 1) * 128],
                                         v_sb[:, p, h * GHV + ec * 128: h * GHV + (ec + 1) * 128], AT[ia][:],
                                         start=(ec == 0 and h % 2 == 0), stop=False)
                            ins = None
                            for ec in range(2):
                                ins = e.matmul(o_ps[:, ec * 128: ec * 128 + CH], Sbf[cur][:, h, ec * 128:(ec + 1) * 128],
                                               qt[:, h, p * 128: p * 128 + CH], start=False, stop=False)
                            return ins
                        P.op("pe", mmo, reads=[Rv[p], RAT[ia], RSbf[cur][h], Rqt[h]],
                             writes=[Ro[h]] + ([Rq_ps if h < 2 else Rk_ps] if p == 0 else []))
                    elif stage == 1 and do_o:
                        o_ps = o_pss[h]

                        def mmo2(e, h=h, p=p, cur=cur, o_ps=o_ps):
                            ins = None
                            for ec in range(2):
                                ins = e.matmul(o_ps[:, ec * 128 + CH:(ec + 1) * 128], Sbf[cur][:, h, ec * 128:(ec + 1) * 128],
                                               qt[:, h, p * 128 + CH:(p + 1) * 128], start=False, stop=True)
                            return ins
                        P.op("pe", mmo2, reads=[RSbf[cur][h], Rqt[h]], writes=[Ro[h]])
                        P.op("act", lambda e, h=h, p=p, ob_=ob_: e.activation(
                            out=ost[ob_][:, 2 * h:2 * h + 2, p * 128:(p + 1) * 128],
                            in_=o_pss3[h], func=AF.Copy),
                            reads=[Ro[h]], writes=[Rost[ob_]])
                    if not do_u:
                        continue
                    iu = iU % 2
                    iU += 1
                    vsrc = v_sb if stage == 0 else v_od
                    P.op("pe", lambda e, h=h, p=p, stage=stage, iu=iu, vsrc=vsrc: e.matmul(
                        ubs[iu], khat[:, p, stage, h, :], vsrc[0:64, p, h * GHV:(h + 1) * GHV],
                        start=True, stop=True), reads=[Rkhat[p][h], Rv[p]], writes=[Rub[iu]])
                    ci = p * 2 + stage
                    P.op("dve", lambda e, h=h, ci=ci, iu=iu: e.scalar_tensor_tensor(
                        out=Sf[:, h, :], in0=Sf[:, h, :], scalar=dch[:, h, ci:ci + 1], in1=ubs[iu],
                        op0=ALU.mult, op1=ALU.add), reads=[RSf[h], Rdch[h], Rub[iu]], writes=[RSf[h]])
                    nxt = (m + 1) % 2
                    P.op("pool", lambda e, h=h, nxt=nxt: e.tensor_copy(out=Sbf[nxt][:, h, :], in_=Sf[:, h, :]),
                         reads=[RSf[h]], writes=[RSbf[nxt][h]])
            mchunk += 2
        if GLA_DBG in (0, 3):
            P.dma("sp", olv[:, :, tsl(tb)], ost[ob_][:], reads=[Rost[ob_]], is_out=True)
    P.dma("sp", A["Sloc"], Sf[:].rearrange("p h e -> p (h e)"), reads=RSf, is_out=True)
    P.op("act", lambda e: e.activation(out=Dt[:], in_=Glast[:], func=AF.Exp, scale=-1.0 / 16),
         reads=[RGlast], writes=[RDt])
    P.dma("sp", A["Dtot"], Dt[:], reads=[RDt], is_out=True)


def gla_b(P, C, A):
    gmix, Rgmix = load_small(P, "gmix_b", A["gmix"], [128, 8])
    gn, Rgn = load_small(P, "gnorm", A["gnorm"], [128, 2])
    sel, Rsel = load_small(P, "sel", A["sel"], [128, NCORES])
    Dall, RDall = load_small(P, "Dall", A["Dall"].rearrange("j p h -> p j h"), [128, NCORES, 4])
    Sall = P.sb("Sall", [128, NCORES, 4 * GHV], F32)
    RSall = P.res("Sall")
    for j in range(NCORES):
        P.dma("sp", Sall[:, j, :], A["Sall"][j], writes=[RSall])
    wg, Rwg = load_w_bf16(P, "gw_g", A["gw_in"], 8, GDV, [(0, 2048, GDV)])
    wo, Rwo = load_w_bf16(P, "gw_o", A["w_o"], 8, DM)
    T = P.sb("Tst", [128, 4, GHV], F32)
    Sin = P.sb("Sin", [128, 4, GHV], F32)
    Sinb = P.sb("Sinb", [128, 4, GHV], BF16)
    RT = [P.res("T") for _ in range(4)]
    RSin = [P.res("Sin") for _ in range(4)]
    RSinb = P.res("Sinb")
    for h in range(4):
        P.op("dve", lambda e, h=h: e.memset(T[:, h, :], 0.0), writes=[RT[h]])
        P.op("dve", lambda e, h=h: e.memset(Sin[:, h, :], 0.0), writes=[RSin[h]])
    for j in range(NCORES - 1):
        for h in range(4):
            P.op("dve", lambda e, h=h, j=j: e.scalar_tensor_tensor(
                out=T[:, h, :], in0=T[:, h, :], scalar=Dall[:, j, h:h + 1], in1=Sall[:, j, h * GHV:(h + 1) * GHV],
                op0=ALU.mult, op1=ALU.add), reads=[RT[h], RDall, RSall], writes=[RT[h]])
            P.op("dve", lambda e, h=h, j=j: e.scalar_tensor_tensor(
                out=Sin[:, h, :], in0=T[:, h, :], scalar=sel[:, j + 1:j + 2], in1=Sin[:, h, :],
                op0=ALU.mult, op1=ALU.add), reads=[RT[h], RSin[h], Rsel], writes=[RSin[h]])
    P.op("dve", lambda e: e.tensor_copy(out=Sinb[:], in_=Sin[:]), reads=RSin, writes=[RSinb])
    ol = [P.sb(f"ol{i}", [128, 8, TB], BF16) for i in range(2)]
    Rol = [P.res("ol") for _ in range(2)]
    qh = [P.sb(f"qhb{i}", [128, 4, TB], BF16) for i in range(2)]
    Rqh = [P.res("qhb") for _ in range(2)]
    of = P.sb("of", [128, 2, TB], F32)
    Rof = P.res("of")
    sg = P.sb("sg", [128, TB], F32)
    Rsg = P.res("sg")
    yin = P.sb("yin", [128, 8, TB], BF16)
    Ryin = P.res("yin")
    c_ps = [P.ps(f"c_ps{i}", [128, TB]) for i in range(2)]
    Rc_ps = [P.res("c_ps") for _ in range(2)]
    g_ps = P.ps("gb_ps", [128, TB])
    Rg_ps = P.res("gb_ps")
    yps = [P.ps(f"gyps{i}", [128, TB]) for i in range(2)]
    Ryps = [P.res("gyps") for _ in range(2)]
    olv = A["oloc"].rearrange("(c p) t -> p c t", p=128)
    qhv = A["qhat"].rearrange("(c p) t -> p c t", p=128)
    for tb in range(NB):
        b = tb % 2
        P.dma("sp", ol[b][:], olv[:, :, tsl(tb)], writes=[Rol[b]])
        P.dma("sp", qh[b][:], qhv[:, :, tsl(tb)], writes=[Rqh[b]])
        norm_x(P, C, tb, gmix, Rgmix, b)
        xn = [C.xn[b][:, c, :] for c in range(8)]
        for h in range(4):
            for ec in range(2):
                cc = 2 * h + ec
                P.op("pe", lambda e, h=h, ec=ec, b=b: e.matmul(c_ps[ec][:], Sinb[:, h, ec * 128:(ec + 1) * 128], qh[b][:, h, :],
                                                          start=True, stop=True), reads=[RSinb, Rqh[b]], writes=[Rc_ps[ec]])
                P.op("dve", lambda e, ec=ec, cc=cc, b=b: e.tensor_tensor(out=of[:, ec, :], in0=ol[b][:, cc, :], in1=c_ps[ec][:],
                                                                   op=ALU.add), reads=[Rol[b], Rc_ps[ec]], writes=[Rof])
            rms_fm(P, C, [of[:, 0, :], of[:, 1, :]], Rof, gn, Rgn, float(GHV), [of[:, 0, :], of[:, 1, :]], Rof)
            for ec in range(2):
                cc = 2 * h + ec
                P.op("pe", lambda e, cc=cc, xn=xn: lin_fm(e, g_ps[:], wg, cc * 128, 128, xn), reads=[C.Rxn[b], Rwg], writes=[Rg_ps])
                P.op("act", lambda e: e.activation(out=sg[:], in_=g_ps[:], func=AF.Silu), reads=[Rg_ps], writes=[Rsg])
                P.op("dve", lambda e, ec=ec, cc=cc: e.tensor_tensor(out=yin[:, cc, :], in0=of[:, ec, :], in1=sg[:], op=ALU.mult),
                     reads=[Rof, Rsg], writes=[Ryin])
        for oc in range(8):
            pb = oc % 2
            P.op("pe", lambda e, oc=oc, pb=pb: lin_fm(e, yps[pb][:], wo, oc * 128, 128, [yin[:, c, :] for c in range(8)]),
                 reads=[Ryin, Rwo], writes=[Ryps[pb]])
            P.op("dve", lambda e, oc=oc, pb=pb, tb=tb: e.tensor_tensor(
                out=C.x[:, oc, tsl(tb)], in0=C.x[:, oc, tsl(tb)], in1=yps[pb][:], op=ALU.add),
                reads=[C.Rx[tb], Ryps[pb]], writes=[C.Rx[tb]])


def decl_mla_pre(nc):
    return {"gmix": din(nc, "p_gmix", [128, 8], F32), "w_in": din(nc, "p_w_in", [DM, 704], F32),
            "qg": din(nc, "p_qg", [128, 3], F32), "kvg": din(nc, "p_kvg", [128, 2], F32),
            "pos": din(nc, "p_pos", [1, NT], I32), "invf": din(nc, "p_invf", [64, 1], F32),
            "sgn": din(nc, "p_sgn", [64, 1], F32),
            "lat": dout(nc, "p_lat", [704, NT], BF16), "tabs": dout(nc, "p_tabs", [128, NT], F32)}


def decl_mlp(nc):
    return {"gmlp": din(nc, "m_g", [128, 8], F32), "w_up": din(nc, "m_w_up", [DM, DFF], F32),
            "w_down": din(nc, "m_w_down", [DFF, DM], F32)}


def decl_gla_a(nc):
    return {"gmix": din(nc, "ga_gmix", [128, 8], F32), "gw_in": din(nc, "ga_w_in", [DM, 3088], F32),
            "w_gk": din(nc, "ga_w_gk", [GRANK, GDK], F32), "b_gk": din(nc, "ga_b_gk", [128, 4], F32),
            "oloc": dout(nc, "ga_oloc", [GDV, NT], BF16), "qhat": dout(nc, "ga_qhat", [GDK, NT], BF16),
            "Sloc": dout(nc, "ga_Sloc", [128, 4 * GHV], F32), "Dtot": dout(nc, "ga_Dtot", [128, 4], F32)}


def decl_gla_b(nc):
    return {"gmix": din(nc, "gb_gmix", [128, 8], F32), "gnorm": din(nc, "gb_gnorm", [128, 2], F32),
            "sel": din(nc, "gb_sel", [128, NCORES], F32), "Dall": din(nc, "gb_Dall", [NCORES, 128, 4], F32),
            "Sall": din(nc, "gb_Sall", [NCORES, 128, 4 * GHV], F32), "gw_in": din(nc, "gb_w_in", [DM, 3088], F32),
            "w_o": din(nc, "gb_w_o", [DM, DM], F32), "oloc": din(nc, "gb_oloc", [GDV, NT], BF16),
            "qhat": din(nc, "gb_qhat", [GDK, NT], BF16)}


def build_tok(kind):
    nc = bass.Bass("TRN2", target_bir_lowering=False)
    xT = din(nc, "xT", [DM, NT], F32)
    P = Prog(nc)
    C = Ctx()
    tok_common(P, C)
    load_x(P, C, xT)

    def phase(fn, A):
        P.phase_begin()
        fn(P, C, A)
        P.phase_end()
    if kind == "T1":
        phase(mla_pre, decl_mla_pre(nc))
    elif kind == "T2":
        A = {"oT": din(nc, "a_oT", [DM, NT], BF16), "w_o": din(nc, "a_w_o", [DM, DM], F32)}
        phase(post_attn, A)
        phase(mlp, decl_mlp(nc))
        phase(gla_a, decl_gla_a(nc))
    elif kind == "T3":
        phase(gla_b, decl_gla_b(nc))
        phase(mlp, decl_mlp(nc))
        phase(mla_pre, decl_mla_pre(nc))
    elif kind == "T5":
        phase(gla_b, decl_gla_b(nc))
        phase(mlp, decl_mlp(nc))
        phase(final_norm, {"gfin": din(nc, "f_g", [128, 8], F32), "out": dout(nc, "f_out", [DM, NT], F32)})
    elif kind == "GB":
        phase(gla_b, decl_gla_b(nc))
    elif kind == "M":
        phase(mlp, decl_mlp(nc))
    elif kind == "F":
        phase(final_norm, {"gfin": din(nc, "f_g", [128, 8], F32), "out": dout(nc, "f_out", [DM, NT], F32)})
    if kind in ("T2", "T3", "GB", "M"):
        xo = dout(nc, "xT_out", [DM, NT], F32)
        store_x(P, C, xo)
    P.finish()
    return nc


def g128(v):
    v = np.asarray(v, dtype=np.float32)
    return np.ascontiguousarray(v.reshape(-1, 128).T)


_INVF = (1.0 / (10000.0 ** (np.arange(0, ROPE, 2, dtype=np.float32) / ROPE))).astype(np.float32)
_INVF2 = np.ascontiguousarray(np.concatenate([_INVF, _INVF])[:, None])
_SGN = np.ascontiguousarray(np.concatenate([-np.ones(32), np.ones(32)]).astype(np.float32)[:, None])

_PROGS = {}


def get_prog(kind):
    if kind not in _PROGS:
        _PROGS[kind] = build_attn() if kind == "H" else build_tok(kind)
    return _PROGS[kind]


def run(kind, maps):
    res = run_bass_kernel_spmd(get_prog(kind), maps, core_ids=list(range(NCORES)))
    return res.results


def mla_pre_inputs(I, l, j, c):
    return {"p_gmix": g128(I["norm_mix"][l]), "p_w_in": np.ascontiguousarray(I["mla_w_in"][j]),
            "p_qg": g128(I["mla_q_norm"][j]), "p_kvg": g128(I["mla_kv_norm"][j]),
            "p_pos": np.ascontiguousarray(I["positions"][:, c * NT:(c + 1) * NT]).astype(np.int32),
            "p_invf": _INVF2, "p_sgn": _SGN}


def mlp_inputs(I, l):
    return {"m_g": g128(I["norm_mlp"][l]), "m_w_up": np.ascontiguousarray(I["mlp_w_up"][l]),
            "m_w_down": np.ascontiguousarray(I["mlp_w_down"][l])}


def gla_a_inputs(I, l, j):
    return {"ga_gmix": g128(I["norm_mix"][l]), "ga_w_in": np.ascontiguousarray(I["gla_w_in"][j]),
            "ga_w_gk": np.ascontiguousarray(I["gla_w_gk_up"][j]), "ga_b_gk": g128(I["gla_b_gk"][j])}


def gla_b_inputs(I, l, j, c, prev):
    sel = np.zeros((128, NCORES), np.float32)
    sel[:, c] = 1.0
    return {"gb_gmix": g128(I["norm_mix"][l]), "gb_gnorm": g128(I["gla_g_norm"][j]), "gb_sel": sel,
            "gb_Dall": np.ascontiguousarray(np.stack([prev[k]["ga_Dtot"] for k in range(NCORES)])),
            "gb_Sall": np.ascontiguousarray(np.stack([prev[k]["ga_Sloc"] for k in range(NCORES)])),
            "gb_w_in": np.ascontiguousarray(I["gla_w_in"][j]), "gb_w_o": np.ascontiguousarray(I["gla_w_o"][j]),
            "gb_oloc": prev[c]["ga_oloc"], "gb_qhat": prev[c]["ga_qhat"]}


def attn_inputs(I, j, pre):
    lat = np.concatenate([pre[k]["p_lat"] for k in range(NCORES)], axis=1)
    tabs = np.concatenate([pre[k]["p_tabs"] for k in range(NCORES)], axis=1)
    latq = np.ascontiguousarray(lat[0:QL])
    latkv = np.ascontiguousarray(lat[QL:QL + KVL])
    latr = np.ascontiguousarray(lat[QL + KVL:])
    cos2 = np.ascontiguousarray(tabs[0:64])
    sin2 = np.ascontiguousarray(tabs[64:128])
    maps = []
    wuq = I["mla_w_uq"][j]
    wukv = I["mla_w_ukv"][j]
    for h in range(NCORES):
        q0 = h * (NOPE + ROPE)
        wr = wuq[:, q0 + NOPE:q0 + NOPE + ROPE]
        wq = np.concatenate([wuq[:, q0:q0 + NOPE], wr, wr[:, 32:], wr[:, :32]], axis=1)
        k0 = h * (NOPE + VH)
        wkv = wukv[:, k0:k0 + NOPE + VH]
        maps.append({"latq": latq, "latkv": latkv, "latr": latr, "cos2": cos2, "sin2": sin2,
                     "wq": np.ascontiguousarray(wq), "wkv": np.ascontiguousarray(wkv)})
    return maps


def kernel(**I):
    I = {k: np.asarray(v) for k, v in I.items()}
    x = I["x"][0]
    xT = [np.ascontiguousarray(x[c * NT:(c + 1) * NT].T) for c in range(NCORES)]
    pre = run("T1", [dict(xT=xT[c], **mla_pre_inputs(I, 0, 0, c)) for c in range(NCORES)])
    for l in (0, 2):
        j = l // 2
        att = run("H", attn_inputs(I, j, pre))
        oT = np.concatenate([att[h]["oT"] for h in range(NCORES)], axis=0)
        r2 = run("T2", [dict(xT=xT[c], a_oT=np.ascontiguousarray(oT[:, c * NT:(c + 1) * NT]),
                             a_w_o=np.ascontiguousarray(I["mla_w_o"][j]),
                             **mlp_inputs(I, l), **gla_a_inputs(I, l + 1, j)) for c in range(NCORES)])
        xT = [r2[c]["xT_out"] for c in range(NCORES)]
        rb = run("GB", [dict(xT=xT[c], **gla_b_inputs(I, l + 1, j, c, r2)) for c in range(NCORES)])
        xT = [rb[c]["xT_out"] for c in range(NCORES)]
        rm = run("M", [dict(xT=xT[c], **mlp_inputs(I, l + 1)) for c in range(NCORES)])
        xT = [rm[c]["xT_out"] for c in range(NCORES)]
        if l == 0:
            pre = run("T1", [dict(xT=xT[c], **mla_pre_inputs(I, 2, 1, c)) for c in range(NCORES)])
        else:
            rf = run("F", [dict(xT=xT[c], f_g=g128(I["final_norm"])) for c in range(NCORES)])
            out = np.concatenate([rf[c]["f_out"].T for c in range(NCORES)], axis=0)
            return np.ascontiguousarray(out[None].astype(np.float32))
```
